# Optimizing a Trainium2 kernel written in Bass

```python
import math
import jax, jax.numpy as jnp
from jax import lax
import numpy as np

D_MODEL = 4096
BATCH = 2
SEQ = 8192
DEPTH = 2

N_A_LAYERS = DEPTH // 2
N_B_LAYERS = DEPTH - N_A_LAYERS
HEAD_DIM = 128
DIFF_HEADS = D_MODEL // (2 * HEAD_DIM)
DIFF_MAPS = 2 * DIFF_HEADS
SWA_Q_HEADS = D_MODEL // HEAD_DIM
SWA_KV_HEADS = SWA_Q_HEADS // 4
SWA_GROUP = SWA_Q_HEADS // SWA_KV_HEADS
WINDOW = 128
BLOCK = 128
D_FF = -(-8 * D_MODEL // (3 * 256)) * 256
NUM_BUCKETS = 32
MAX_DISTANCE = 128
BIAS_HEADS = SWA_Q_HEADS
EPS = 1e-6
NEG = -1e30

kernel_name = 'hybrid_diffattn_swa_sinks_yoco'


def rms_norm(x, g):
    xf = x.astype(jnp.float32)
    y = xf * lax.rsqrt(jnp.mean(xf * xf, axis=-1, keepdims=True) + EPS)
    return (y * g.astype(jnp.float32)).astype(x.dtype)


def t5_bucket(n):
    max_exact = NUM_BUCKETS // 2
    nf = jnp.maximum(n, 1).astype(jnp.float32)
    large = max_exact + (jnp.log(nf / max_exact) / math.log(MAX_DISTANCE / max_exact)
                         * (NUM_BUCKETS - max_exact)).astype(jnp.int32)
    large = jnp.minimum(large, NUM_BUCKETS - 1)
    return jnp.where(n < max_exact, n, large)


def distance_bias_table(rel_bias):
    return rel_bias[t5_bucket(jnp.arange(MAX_DISTANCE))]


def bias_from_distance(lut, dist):
    return jnp.moveaxis(lut[jnp.clip(dist, 0, MAX_DISTANCE - 1)], -1, 0).astype(jnp.float32)


def swiglu(h, w_gate, w_up, w_down):
    return (jax.nn.silu(h @ w_gate) * (h @ w_up)) @ w_down


def diff_attention(h, w_qkv, w_o, g_q, g_k, lam_qk, g_sub, lut, lambda_init):
    B, S, _ = h.shape
    nb = S // BLOCK
    q, k, v = jnp.split(h @ w_qkv, 3, axis=-1)
    q = rms_norm(q.reshape(B, S, DIFF_MAPS, HEAD_DIM), g_q)
    k = rms_norm(k.reshape(B, S, DIFF_MAPS, HEAD_DIM), g_k)
    v = v.reshape(B, S, DIFF_HEADS, 2 * HEAD_DIM)
    lf = lam_qk.astype(jnp.float32)
    lam = jnp.exp(jnp.sum(lf[0] * lf[1])) - jnp.exp(jnp.sum(lf[2] * lf[3])) + lambda_init
    scale = HEAD_DIM ** -0.5
    q_blocks = q.reshape(B, nb, BLOCK, DIFF_MAPS, HEAD_DIM).transpose(1, 0, 2, 3, 4)
    key_pos = jnp.arange(S)

    def one_block(args):
        qb, bi = args
        logits = jnp.einsum('bqmd,bkmd->bmqk', qb, k,
                            preferred_element_type=jnp.float32) * scale
        dist = (bi * BLOCK + jnp.arange(BLOCK))[:, None] - key_pos[None, :]
        logits = logits + bias_from_distance(lut, dist)[None]
        logits = jnp.where(dist >= 0, logits, NEG)
        p = jax.nn.softmax(logits, axis=-1).reshape(B, DIFF_HEADS, 2, BLOCK, S)
        a = p[:, :, 0] - lam * p[:, :, 1]
        return jnp.einsum('bhqk,bkhe->bqhe', a.astype(v.dtype), v)

    o = lax.map(one_block, (q_blocks, jnp.arange(nb)))
    o = o.transpose(1, 0, 2, 3, 4).reshape(B, S, DIFF_HEADS, 2 * HEAD_DIM)
    o = rms_norm(o, g_sub) * (1.0 - lambda_init)
    return o.reshape(B, S, D_MODEL) @ w_o


def swa_sinks_attention(h, w_q, w_o, g_q, k, v, sinks, lut):
    B, S, _ = h.shape
    nb = S // BLOCK
    q = rms_norm((h @ w_q).reshape(B, S, SWA_Q_HEADS, HEAD_DIM), g_q)
    q = q.reshape(B, nb, BLOCK, SWA_KV_HEADS, SWA_GROUP, HEAD_DIM)

    def with_prev(t):
        t = t.reshape(B, nb, BLOCK, SWA_KV_HEADS, HEAD_DIM)
        prev = jnp.pad(t[:, :-1], ((0, 0), (1, 0), (0, 0), (0, 0), (0, 0)))
        return jnp.concatenate([prev, t], axis=2)

    kk, vv = with_prev(k), with_prev(v)
    logits = jnp.einsum('bnqgrd,bnkgd->bngrqk', q, kk,
                        preferred_element_type=jnp.float32) * (HEAD_DIM ** -0.5)
    qi = jnp.arange(BLOCK)[:, None]
    kj = jnp.arange(2 * BLOCK)[None, :]
    dist = BLOCK + qi - kj
    bias = bias_from_distance(lut, dist).reshape(SWA_KV_HEADS, SWA_GROUP, BLOCK, 2 * BLOCK)
    logits = logits + bias
    valid = ((dist >= 0) & (dist < WINDOW))[None] & (
        (jnp.arange(nb)[:, None, None] > 0) | (kj[None] >= BLOCK))
    logits = jnp.where(valid[None, :, None, None], logits, NEG)
    sink = jnp.broadcast_to(
        sinks.astype(jnp.float32).reshape(1, 1, SWA_KV_HEADS, SWA_GROUP, 1, 1),
        logits.shape[:-1] + (1,))
    p = jax.nn.softmax(jnp.concatenate([logits, sink], axis=-1), axis=-1)[..., :-1]
    o = jnp.einsum('bngrqk,bnkgd->bnqgrd', p.astype(vv.dtype), vv)
    return o.reshape(B, S, D_MODEL) @ w_o


def setup_inputs(seed: int = 0) -> dict:
    key = jax.random.key(seed)
    ks = jax.random.split(key, 24)
    f32 = jnp.float32

    def w(k, shape, fan_in):
        return jax.random.normal(k, shape, f32) * (fan_in ** -0.5)

    def gain(k, shape):
        return 1.0 + 0.02 * jax.random.normal(k, shape, f32)

    kv_width = SWA_KV_HEADS * HEAD_DIM
    return {
        'x': jax.random.normal(ks[0], (BATCH, SEQ, D_MODEL), f32),
        'rel_bias': 0.5 * jax.random.normal(ks[1], (NUM_BUCKETS, BIAS_HEADS), f32),
        'g_attn_norm': gain(ks[2], (DEPTH, D_MODEL)),
        'g_ffn_norm': gain(ks[3], (DEPTH, D_MODEL)),
        'w_qkv_a': w(ks[4], (N_A_LAYERS, D_MODEL, 3 * D_MODEL), D_MODEL),
        'w_o_a': w(ks[5], (N_A_LAYERS, D_MODEL, D_MODEL), D_MODEL),
        'g_q_a': gain(ks[6], (N_A_LAYERS, HEAD_DIM)),
        'g_k_a': gain(ks[7], (N_A_LAYERS, HEAD_DIM)),
        'lam_qk_a': 0.1 * jax.random.normal(ks[8], (N_A_LAYERS, 4, HEAD_DIM), f32),
        'g_sub_a': gain(ks[9], (N_A_LAYERS, 2 * HEAD_DIM)),
        'g_kv_norm': gain(ks[10], (D_MODEL,)),
        'w_kv': w(ks[11], (D_MODEL, 2 * kv_width), D_MODEL),
        'g_k_shared': gain(ks[12], (HEAD_DIM,)),
        'w_q_b': w(ks[13], (N_B_LAYERS, D_MODEL, D_MODEL), D_MODEL),
        'w_o_b': w(ks[14], (N_B_LAYERS, D_MODEL, D_MODEL), D_MODEL),
        'g_q_b': gain(ks[15], (N_B_LAYERS, HEAD_DIM)),
        'sinks_b': 0.5 * jax.random.normal(ks[16], (N_B_LAYERS, SWA_Q_HEADS), f32),
        'w_gate': w(ks[17], (DEPTH, D_MODEL, D_FF), D_MODEL),
        'w_up': w(ks[18], (DEPTH, D_MODEL, D_FF), D_MODEL),
        'w_down': w(ks[19], (DEPTH, D_FF, D_MODEL), D_FF),
    }


def reference(x, rel_bias, g_attn_norm, g_ffn_norm, w_qkv_a, w_o_a, g_q_a, g_k_a,
              lam_qk_a, g_sub_a, g_kv_norm, w_kv, g_k_shared, w_q_b, w_o_b, g_q_b,
              sinks_b, w_gate, w_up, w_down):
    B, S, _ = x.shape
    lut = distance_bias_table(rel_bias)
    k_sh = None
    v_sh = None
    for i in range(DEPTH):
        h = rms_norm(x, g_attn_norm[i])
        if i < N_A_LAYERS:
            a = i
            lambda_init = 0.8 - 0.6 * math.exp(-0.3 * i)
            x = x + diff_attention(h, w_qkv_a[a], w_o_a[a], g_q_a[a], g_k_a[a],
                                   lam_qk_a[a], g_sub_a[a], lut, lambda_init)
        else:
            b = i - N_A_LAYERS
            x = x + swa_sinks_attention(h, w_q_b[b], w_o_b[b], g_q_b[b],
                                        k_sh, v_sh, sinks_b[b], lut)
        x = x + swiglu(rms_norm(x, g_ffn_norm[i]), w_gate[i], w_up[i], w_down[i])
        if i == N_A_LAYERS - 1:
            k_sh, v_sh = jnp.split(rms_norm(x, g_kv_norm) @ w_kv, 2, axis=-1)
            k_sh = rms_norm(k_sh.reshape(B, S, SWA_KV_HEADS, HEAD_DIM), g_k_shared)
            v_sh = v_sh.reshape(B, S, SWA_KV_HEADS, HEAD_DIM)
    return x
```

```python
import math
import contextlib
import itertools
import numpy as np
import concourse.bass as bass
import concourse.mybir as mybir
from concourse.bass_utils import run_bass_kernel_spmd

F32 = mybir.dt.float32
BF16 = mybir.dt.bfloat16
AF = mybir.ActivationFunctionType
ALU = mybir.AluOpType
AX = mybir.AxisListType

EPS = 1e-6
NUM_BUCKETS = 32
MAX_DISTANCE = 128
PW = 256
VAW = 260
VSW = 132


class Cfg:
    def __init__(self, D=4096, DFF=11008, NQ=16, P0_TB=8, NSLOT=3):
        self.D = D
        self.DFF = DFF
        self.NQ = NQ
        self.NB = 4 * NQ
        self.NA = NQ + 1
        self.NCH = D // 128
        self.FCH = DFF // 128
        self.MAPS = D // 128
        self.H0 = D // 256
        self.QH = D // 128
        self.KVH = self.QH // 4
        self.P0_TB = P0_TB
        self.NSLOT = NSLOT
        assert D % 256 == 0 and DFF % 256 == 0 and self.QH % 4 == 0
        assert (self.KVH * 128) % PW == 0 or self.KVH * 128 == 128


def t5_bucket_table():
    n = np.arange(MAX_DISTANCE)
    max_exact = NUM_BUCKETS // 2
    nf = np.maximum(n, 1).astype(np.float32)
    large = max_exact + (np.log(nf / np.float32(max_exact)) / np.float32(math.log(MAX_DISTANCE / max_exact))
                         * np.float32(NUM_BUCKETS - max_exact)).astype(np.int32)
    large = np.minimum(large, NUM_BUCKETS - 1)
    return np.where(n < max_exact, n, large).astype(np.int64)


class Buf:
    __slots__ = ("ap", "name", "w", "r", "dsem", "dcnt", "accum", "union", "lo", "hi", "excl")

    def __init__(self, ap, name, accum=False, union=None, lo=0, hi=0):
        self.ap = ap
        self.name = name
        self.w = {}
        self.r = {}
        self.dsem = None
        self.dcnt = 0
        self.accum = accum
        self.union = union
        self.excl = False
        self.lo = lo
        self.hi = hi
        if union is not None:
            union.append(self)


class Eng:
    def __init__(self, h, name, sem):
        self.h = h
        self.name = name
        self.sem = sem
        self.cnt = 0
        self.seen = {}


def _merge(d, s):
    for k_, v in s.items():
        if d.get(k_, 0) < v:
            d[k_] = v


class K:
    def __init__(self, nc, es):
        self.nc = nc
        self.es = es
        self.sems = []
        self.PE = Eng(nc.tensor, "pe", self.sem("e_pe"))
        self.ACT = Eng(nc.scalar, "act", self.sem("e_act"))
        self.DVE = Eng(nc.vector, "dve", self.sem("e_dve"))
        self.POOL = Eng(nc.gpsimd, "pool", self.sem("e_pool"))
        self.SP = Eng(nc.sync, "sp", None)
        self.engs = [self.PE, self.ACT, self.DVE, self.POOL, self.SP]
        self.dbufs = []
        self.uid = 0

    def sem(self, name):
        h = self.es.enter_context(self.nc.semaphore(name))
        self.sems.append(h)
        return len(self.sems) - 1

    def name(self, p):
        self.uid += 1
        return f"{p}_{self.uid}"

    def sb(self, es, shape, dt, name):
        t = es.enter_context(self.nc.sbuf_tensor(self.name(name), list(shape), dt))
        return Buf(t[:] if False else t, name)

    def wait(self, eng, deps):
        for s, v in deps.items():
            if eng.seen.get(s, 0) < v:
                eng.h.wait_ge(self.sems[s], v)
                eng.seen[s] = v

    def _deps(self, reads, writes):
        deps = {}
        for b in reads:
            _merge(deps, b.w)
            if b.excl:
                _merge(deps, b.r)
        for b in writes:
            if not b.accum:
                _merge(deps, b.w)
                _merge(deps, b.r)
                if b.union is not None:
                    for o in b.union:
                        if o is not b and o.lo < b.hi and b.lo < o.hi:
                            _merge(deps, o.w)
                            _merge(deps, o.r)
        return deps

    def _commit(self, reads, writes, s, v):
        for b in reads:
            if b.r.get(s, 0) < v:
                b.r[s] = v
        for b in writes:
            if b.accum:
                if b.w.get(s, 0) < v:
                    b.w[s] = v
            else:
                b.w = {s: v}
                b.r = {}

    def op(self, eng, thunk, reads=(), writes=()):
        self.wait(eng, self._deps(reads, writes))
        ins = thunk()
        eng.cnt += 1
        ins.then_inc(self.sems[eng.sem], 1)
        self._commit(reads, writes, eng.sem, eng.cnt)

    def dma(self, q, pairs, sbuf, reads=(), writes=()):
        if sbuf.dsem is None:
            sbuf.dsem = self.sem(self.name("d_" + sbuf.name))
            self.dbufs.append(sbuf)
        deps = self._deps(reads, writes)
        if sbuf.dcnt:
            _merge(deps, {sbuf.dsem: 16 * sbuf.dcnt})
        self.wait(q, deps)
        for (o, i) in pairs:
            q.h.dma_start(out=o, in_=i).then_inc(self.sems[sbuf.dsem], 16)
            sbuf.dcnt += 1
        self._commit(reads, writes, sbuf.dsem, 16 * sbuf.dcnt)

    def barrier(self):
        deps = {}
        for e in self.engs:
            if e.sem is not None and e.cnt:
                deps[e.sem] = e.cnt
        for b in self.dbufs:
            if b.dcnt:
                deps[b.dsem] = 16 * b.dcnt
        for e in self.engs:
            self.wait(e, deps)


class WStream:
    def __init__(self, k, es, cfg, plan):
        self.k = k
        self.n = cfg.NSLOT
        self.slots = []
        for i in range(self.n):
            t = es.enter_context(k.nc.sbuf_tensor(f"wslot{i}", [128, 32, PW], BF16))
            self.slots.append(Buf(t, f"wslot{i}"))
        self.plan = iter(plan)
        self.issued = 0
        self.consumed = 0
        self.descs = []

    def _issue(self):
        d = next(self.plan, None)
        if d is None:
            self.issued += 1
            self.descs.append(None)
            return
        w, k0, nk, c0, ncols = d
        slot = self.slots[self.issued % self.n]
        nch = nk // 128
        pairs = []
        g = 0
        while g < nch:
            ng = min(8, nch - g)
            src = w[k0 + g * 128:k0 + (g + ng) * 128, c0:c0 + ncols].rearrange("(g p) n -> p g n", p=128)
            pairs.append((slot.ap[:, g:g + ng, 0:ncols], src))
            g += ng
        self.k.dma(self.k.POOL, pairs, slot, writes=[slot])
        self.issued += 1
        self.descs.append(d)

    def acquire(self, expect):
        while self.issued < self.consumed + self.n:
            self._issue()
        d = self.descs[self.consumed]
        assert d is not None and (d[1], d[2], d[3], d[4]) == tuple(expect[1:]), (d, expect)
        assert d[0].tensor.name == expect[0].tensor.name and d[0].offset == expect[0].offset, (d, expect)
        slot = self.slots[self.consumed % self.n]
        self.consumed += 1
        return slot


def active_tiles(cfg):
    tiles = [[0]]
    a = 1
    while a < cfg.NA:
        tiles.append(list(range(a, min(a + 4, cfg.NA))))
        a += 4
    return tiles


def p0_tiles(cfg):
    tiles = []
    b = 0
    while b < cfg.NB:
        tiles.append(list(range(b, min(b + cfg.P0_TB, cfg.NB))))
        b += cfg.P0_TB
    return tiles


def subtiles(blocks):
    return [blocks[i:i + 4] for i in range(0, len(blocks), 4)]


def build_program(cfg, debug_phase=99):
    nc = bass.Bass("TRN2", target_bir_lowering=False)
    D, DFF, NB, NQ, NA, NCH, FCH = cfg.D, cfg.DFF, cfg.NB, cfg.NQ, cfg.NA, cfg.NCH, cfg.FCH
    MAPS, H0, QH, KVH = cfg.MAPS, cfg.H0, cfg.QH, cfg.KVH
    A0 = NB - NA
    KVW = KVH * 128
    scale = 128 ** -0.5
    lambda_init0 = 0.8 - 0.6 * math.exp(-0.3 * 0)

    def din(name, shape, dt=F32):
        return nc.dram_tensor(name, list(shape), dt, kind="ExternalInput").ap()

    def dscr(name, shape, dt):
        kind = "ExternalOutput" if getattr(cfg, "debug", False) else "Internal"
        return nc.dram_tensor(name, list(shape), dt, kind=kind).ap()

    xloc = din("xloc", [NB * 128, D])
    valid = din("valid", [128, NB])
    rel_bias = din("rel_bias", [NUM_BUCKETS, MAPS])
    g_attn = din("g_attn_norm", [2, D])
    g_ffn = din("g_ffn_norm", [2, D])
    w_qkv = din("w_qkv_a", [D, 3 * D])
    w_o_a = din("w_o_a", [D, D])
    g_q_a = din("g_q_a", [128, 1])
    g_k_a = din("g_k_a", [128, 1])
    lam_qk = din("lam_qk_a", [1, 512])
    g_sub = din("g_sub_a", [1, 256])
    g_kvn = din("g_kv_norm", [1, D])
    w_kv = din("w_kv", [D, 2 * KVW])
    g_k_sh = din("g_k_shared", [128, 1])
    w_q_b = din("w_q_b", [D, D])
    w_o_b = din("w_o_b", [D, D])
    g_q_b = din("g_q_b", [128, 1])
    sinks = din("sinks_b", [1, QH])
    w_gate = din("w_gate", [2, D, DFF])
    w_up = din("w_up", [2, D, DFF])
    w_down = din("w_down", [2, DFF, D])
    out = nc.dram_tensor("out", [NQ * 128, D], F32, kind="ExternalOutput").ap()

    XNT = dscr("XNT", [NCH, 128, NB * 128], BF16)
    KT0 = dscr("KT0", [MAPS, 128, NB * 128], BF16)
    VA0 = dscr("VA0", [H0, NB * 128, VAW], BF16)
    QT0 = dscr("QT0", [MAPS, 128, NA * 128], BF16)
    O0 = dscr("O0", [NA * 128, D], BF16)
    X1S = dscr("X1S", [NA * 128, D], F32)
    X2 = dscr("X2", [NA * 128, D], F32)
    KSHT = dscr("KSHT", [KVH, 128, NA * 128], BF16)
    VSH = dscr("VSH", [KVH, NA * 128, VSW], BF16)
    X3S = dscr("X3S", [NQ * 128, D], F32)
    GX = dscr("GX", [2, MAPS, 384], F32)
    EB = dscr("EB", [3, 128, MAPS, 128], F32)
    EBB = dscr("EBB", [3, 128, MAPS, 128], BF16)

    bucket = t5_bucket_table()
    atiles = active_tiles(cfg)

    def plan():
        for blocks in p0_tiles(cfg):
            for p in range(D // PW):
                yield (w_qkv, 0, D, D + p * PW, PW)
            for p in range(D // PW):
                yield (w_qkv, 0, D, 2 * D + p * PW, PW)
            if any(b >= A0 for b in blocks):
                for p in range(D // PW):
                    yield (w_qkv, 0, D, p * PW, PW)
        if debug_phase < 2:
            return
        for layer in range(2):
            ntiles = len(atiles) if layer == 0 else NQ // 4 if NQ >= 4 else 1
            for t in range(ntiles):
                if layer == 1:
                    for p in range(D // PW):
                        yield (w_q_b, 0, D, p * PW, PW)
                wo = w_o_a if layer == 0 else w_o_b
                for p in range(D // PW):
                    yield (wo, 0, D, p * PW, PW)
                for p in range(DFF // PW):
                    yield (w_gate[layer], 0, D, p * PW, PW)
                    yield (w_up[layer], 0, D, p * PW, PW)
                for p in range(D // PW):
                    k0 = 0
                    while k0 < DFF:
                        nk = min(32 * 128, DFF - k0)
                        yield (w_down[layer], k0, nk, p * PW, PW)
                        k0 += nk
                if layer == 0:
                    wkv_pw = min(PW, KVW)
                    for p in range(KVW // wkv_pw):
                        yield (w_kv, 0, D, p * wkv_pw, wkv_pw)
                    for p in range(KVW // wkv_pw):
                        yield (w_kv, 0, D, KVW + p * wkv_pw, wkv_pw)
            if debug_phase < 3:
                return

    es = contextlib.ExitStack()
    with es:
        k = K(nc, es)
        PE, ACT, DVE, POOL, SP = k.PE, k.ACT, k.DVE, k.POOL, k.SP
        ws = WStream(k, es, cfg, plan())

        def sbt(stack, shape, dt, name):
            t = stack.enter_context(nc.sbuf_tensor(k.name(name), list(shape), dt))
            return Buf(t, name)

        banks = []
        bank2 = []
        for i in range(4):
            t = es.enter_context(nc.psum_tensor(f"bankpair{i}", [128, 2, 512], F32))
            bank2.append(t)
            for j_ in range(2):
                banks.append(Buf(t[:, j_, :], f"bank{2 * i + j_}"))
                banks[-1].excl = True

        def bank_bf16(b):
            return b.ap.bitcast(BF16)

        dr = {n: Buf(None, n, accum=True) for n in
              ["XNT", "KT0", "VA0", "QT0", "O0", "X1S", "X2", "KSHT", "VSH", "X3S", "GX", "out"]}

        ident = sbt(es, [128, 128], BF16, "ident")
        ones = sbt(es, [128, 128], BF16, "ones")
        gcols = sbt(es, [128, 5, NCH], F32, "gcols")
        gq_a = sbt(es, [128, 1], F32, "gq_a")
        gk_a = sbt(es, [128, 1], F32, "gk_a")
        gk_s = sbt(es, [128, 1], F32, "gk_s")
        gq_b = sbt(es, [128, 1], F32, "gq_b")
        gsub = sbt(es, [128, 256], F32, "gsub")
        valid_sb = sbt(es, [128, NB], F32, "valid_sb")
        cfar = sbt(es, [128, MAPS], F32, "cfar")
        esink = sbt(es, [128, QH], F32, "esink")
        neglam = sbt(es, [128, 1], F32, "neglam")
        eps_col = sbt(es, [128, 1], F32, "eps_col")
        lnli_col = sbt(es, [128, 1], F32, "lnli_col")
        lam_es = contextlib.ExitStack()
        lamw = sbt(lam_es, [128, 512], F32, "lamw")
        lamt = sbt(lam_es, [128, 2], F32, "lamt")
        lamp = sbt(lam_es, [128, 256], F32, "lamp")

        k.op(POOL, lambda: nc.gpsimd.memset(ident.ap[:], 1.0), writes=[ident])
        k.op(POOL, lambda: nc.gpsimd.affine_select(out=ident.ap[:], in_=ident.ap[:], pattern=[[-1, 128]],
                                                   compare_op=ALU.is_equal, fill=0.0, base=0,
                                                   channel_multiplier=1), reads=[ident], writes=[ident])
        k.op(POOL, lambda: nc.gpsimd.memset(ones.ap[:], 1.0), writes=[ones])
        k.op(POOL, lambda: nc.gpsimd.memset(eps_col.ap[:], EPS), writes=[eps_col])
        k.op(POOL, lambda: nc.gpsimd.memset(lnli_col.ap[:], math.log(1.0 - lambda_init0)), writes=[lnli_col])

        with nc.allow_non_contiguous_dma(reason="tiny one-time constant loads"):
            pairs = []
            for i, src in enumerate([g_attn[0:1, :], g_attn[1:2, :], g_ffn[0:1, :], g_ffn[1:2, :], g_kvn]):
                pairs.append((gcols.ap[:, i, :], src.rearrange("o (c p) -> p (o c)", p=128)))
            k.dma(SP, pairs, gcols, writes=[gcols])
            k.dma(SP, [(gq_a.ap[:], g_q_a)], gq_a, writes=[gq_a])
            k.dma(SP, [(gk_a.ap[:], g_k_a)], gk_a, writes=[gk_a])
            k.dma(SP, [(gk_s.ap[:], g_k_sh)], gk_s, writes=[gk_s])
            k.dma(SP, [(gq_b.ap[:], g_q_b)], gq_b, writes=[gq_b])
            k.dma(SP, [(gsub.ap[:], g_sub.partition_broadcast(128))], gsub, writes=[gsub])
            k.dma(SP, [(valid_sb.ap[:], valid)], valid_sb, writes=[valid_sb])
            k.dma(SP, [(cfar.ap[:], rel_bias[int(bucket[127]):int(bucket[127]) + 1, :].partition_broadcast(128))],
                  cfar, writes=[cfar])
            k.dma(SP, [(esink.ap[:], sinks.partition_broadcast(128))], esink, writes=[esink])
            k.dma(SP, [(lamw.ap[:], lam_qk.partition_broadcast(128))], lamw, writes=[lamw])
        k.op(ACT, lambda: nc.scalar.activation(out=esink.ap[:], in_=esink.ap[:], func=AF.Exp),
             reads=[esink], writes=[esink])
        lw = lamw.ap[:].rearrange("p (a b e) -> p a b e", a=2, b=2)
        k.op(DVE, lambda: nc.vector.tensor_tensor(out=lamp.ap[:].rearrange("p (a e) -> p a e", a=2),
                                                  in0=lw[:, :, 0, :], in1=lw[:, :, 1, :], op=ALU.mult),
             reads=[lamw], writes=[lamp])
        k.op(DVE, lambda: nc.vector.tensor_reduce(out=lamt.ap[:], in_=lamp.ap[:].rearrange("p (a e) -> p a e", a=2),
                                                  axis=AX.X, op=ALU.add), reads=[lamp], writes=[lamt])
        k.op(ACT, lambda: nc.scalar.activation(out=lamt.ap[:], in_=lamt.ap[:], func=AF.Exp),
             reads=[lamt], writes=[lamt])
        k.op(DVE, lambda: nc.vector.tensor_tensor(out=neglam.ap[:], in0=lamt.ap[:, 1:2], in1=lamt.ap[:, 0:1],
                                                  op=ALU.subtract), reads=[lamt], writes=[neglam])
        k.op(DVE, lambda: nc.vector.tensor_scalar(out=neglam.ap[:], in0=neglam.ap[:], scalar1=-lambda_init0,
                                                  scalar2=None, op0=ALU.add), reads=[neglam], writes=[neglam])

        k.barrier()
        lam_es.close()
        with contextlib.ExitStack() as cs:
            rbT = sbt(cs, [MAPS, NUM_BUCKETS], F32, "rbT")
            lut = sbt(cs, [MAPS, 128], F32, "lut")
            gx = sbt(cs, [MAPS, 2, 384], F32, "gx")
            with nc.allow_non_contiguous_dma(reason="tiny transposed bias table"):
                k.dma(SP, [(rbT.ap[:], rel_bias.rearrange("b m -> m b"))], rbT, writes=[rbT])
            runs = []
            s = 0
            for d_ in range(1, 129):
                if d_ == 128 or bucket[d_] != bucket[s]:
                    runs.append((s, d_, int(bucket[s])))
                    s = d_
            last = None
            for (s0, s1, bk) in runs:
                def th(s0=s0, s1=s1, bk=bk):
                    return nc.vector.tensor_copy(out=lut.ap[:, s0:s1],
                                                 in_=rbT.ap[:, bk:bk + 1].to_broadcast([MAPS, s1 - s0]))
                k.op(DVE, th, reads=[rbT], writes=[lut])
            k.op(DVE, lambda: nc.vector.memset(gx.ap[:], 0.0), writes=[gx])
            k.op(ACT, lambda: nc.scalar.activation(out=gx.ap[:, 0, 128:256], in_=lut.ap[:], func=AF.Exp),
                 reads=[lut, gx], writes=[gx])
            k.op(ACT, lambda: nc.scalar.activation(out=gx.ap[:, 1, 128:256], in_=lut.ap[:], func=AF.Exp),
                 reads=[lut, gx], writes=[gx])
            k.op(ACT, lambda: nc.scalar.activation(out=gx.ap[:, 0, 256:384],
                                                   in_=lut.ap[:, 127:128].to_broadcast([MAPS, 128]), func=AF.Exp),
                 reads=[lut, gx], writes=[gx])
            k.dma(SP, [(GX.rearrange("t m i -> m t i"), gx.ap[:])], gx, reads=[gx], writes=[dr["GX"]])
            Jm = sbt(cs, [128, 128], F32, "Jm")
            k.op(POOL, lambda: nc.gpsimd.memset(Jm.ap[:], 1.0), writes=[Jm])
            k.op(POOL, lambda: nc.gpsimd.affine_select(out=Jm.ap[:], in_=Jm.ap[:], pattern=[[1, 128]],
                                                       compare_op=ALU.is_equal, fill=0.0, base=-127,
                                                       channel_multiplier=1), reads=[Jm], writes=[Jm])
            hk = [sbt(cs, [128, 4, 128], F32, f"hk{i}") for i in range(2)]
            ebf = [sbt(cs, [128, 4, 128], F32, f"ebf{i}") for i in range(2)]
            ebh = [sbt(cs, [128, 4, 128], BF16, f"ebh{i}") for i in range(2)]
            it = 0
            for kind, (table, off_) in enumerate([(0, 0), (0, 128), (1, 128)]):
                for m0 in range(0, MAPS, 4):
                    nm = min(4, MAPS - m0)
                    a_ = hk[it % 2]
                    f_ = ebf[it % 2]
                    h_ = ebh[it % 2]
                    pb = banks[it % 2]
                    it += 1
                    base = GX[table, m0, 1 + off_:2 + off_]
                    srcap = bass.AP(GX.tensor, base.offset, [[1, 128], [384, nm], [1, 128]])
                    k.dma(SP, [(a_.ap[:, 0:nm, :], srcap)], a_, reads=[dr["GX"]], writes=[a_])
                    k.op(PE, lambda pb=pb, a_=a_, nm=nm: nc.tensor.matmul(
                        pb.ap[:, 0:nm * 128], lhsT=Jm.ap[:], rhs=a_.ap[:, 0:nm, :].rearrange("p m q -> p (m q)"),
                        start=True, stop=True), reads=[Jm, a_], writes=[pb])
                    k.op(ACT, lambda pb=pb, f_=f_, nm=nm: nc.scalar.copy(
                        out=f_.ap[:, 0:nm, :].rearrange("p m q -> p (m q)"), in_=pb.ap[:, 0:nm * 128]),
                        reads=[pb], writes=[f_])
                    k.op(DVE, lambda f_=f_, h_=h_, nm=nm: nc.vector.tensor_copy(
                        out=h_.ap[:, 0:nm, :], in_=f_.ap[:, 0:nm, :]), reads=[f_], writes=[h_])
                    k.dma(SP, [(EB[kind, :, m0:m0 + nm, :], f_.ap[:, 0:nm, :])], f_, reads=[f_], writes=[dr["GX"]])
                    k.dma(SP, [(EBB[kind, :, m0:m0 + nm, :], h_.ap[:, 0:nm, :])], h_, reads=[h_], writes=[dr["GX"]])
            k.barrier()

        def load_eb(dst, kind, m0, nm, bf16=False):
            T = EBB if bf16 else EB
            return (dst, T[kind, :, m0:m0 + nm, :])

        def nt_block(src128, src_buf, gidx, dst_buf, dst_ap_fn, xin, xn, nt_ss, nt_rs, part=0):
            if part in (0, 1):
                nt_block_a(src128, src_buf, xin, xn, nt_ss, nt_rs)
            if part in (0, 2):
                nt_block_b(gidx, dst_buf, dst_ap_fn, xn)

        def nt_block_a(src128, src_buf, xin, xn, nt_ss, nt_rs):
            k.dma(SP, [(xin.ap[:], src128)], xin, reads=[src_buf], writes=[xin])
            k.op(ACT, lambda: nc.scalar.activation(out=xn.ap[:], in_=xin.ap[:], func=AF.Square,
                                                   accum_out=nt_ss.ap[:]), reads=[xin], writes=[xn, nt_ss])
            k.op(ACT, lambda: nc.scalar.activation(out=nt_rs.ap[:], in_=nt_ss.ap[:], func=AF.Ln, scale=1.0 / D,
                                                   bias=eps_col.ap[:, 0:1]), reads=[nt_ss, eps_col], writes=[nt_rs])
            k.op(ACT, lambda: nc.scalar.activation(out=nt_rs.ap[:], in_=nt_rs.ap[:], func=AF.Exp, scale=-0.5),
                 reads=[nt_rs], writes=[nt_rs])
            k.op(DVE, lambda: nc.vector.tensor_scalar(out=xn.ap[:], in0=xin.ap[:], scalar1=nt_rs.ap[:, 0:1],
                                                      scalar2=None, op0=ALU.mult), reads=[xin, nt_rs], writes=[xn])

        def nt_block_b(gidx, dst_buf, dst_ap_fn, xn):
            for grp in range((NCH + 7) // 8):
                c0 = grp * 8
                ncg = min(8, NCH - c0)
                pb = banks[nt_bank[0] % 2 + 6]
                nt_bank[0] += 1
                pv = bank_bf16(pb)[:, 0:ncg * 128].rearrange("p (c t) -> p c t", c=ncg)

                def tr(pv=pv, c0=c0, ncg=ncg):
                    ins = None
                    for c in range(ncg):
                        ins = nc.tensor.transpose(pv[:, c, :], xn.ap[:, (c0 + c) * 128:(c0 + c + 1) * 128],
                                                  ident.ap[:])
                    return ins
                k.op(PE, tr, reads=[xn, ident], writes=[pb])
                gb = gcols.ap[:, gidx, c0:c0 + ncg].unsqueeze(2).to_broadcast([128, ncg, 128])
                dst = dst_ap_fn(c0, ncg)
                k.op(DVE, lambda dst=dst, pv=pv, gb=gb: nc.vector.tensor_tensor(out=dst, in0=pv, in1=gb, op=ALU.mult),
                     reads=[pb, gcols], writes=[dst_buf])

        nt_bank = [0]
        src_rows_buf = [None]

        def headnorm_evac(pbank, ntok, gcol, dstbuf, dst_ap, sq, pss, rs):
            k.op(ACT, lambda: nc.scalar.activation(out=sq.ap[:, 0:ntok], in_=pbank.ap[:, 0:ntok], func=AF.Square),
                 reads=[pbank], writes=[sq])

            def cont():
                k.op(PE, lambda: nc.tensor.matmul(pss.ap[:, 0:ntok], lhsT=ones.ap[:], rhs=sq.ap[:, 0:ntok],
                                                  start=True, stop=True), reads=[sq, ones], writes=[pss])
                k.op(ACT, lambda: nc.scalar.activation(out=rs.ap[:, 0:ntok], in_=pss.ap[:, 0:ntok], func=AF.Ln,
                                                       scale=1.0 / 128, bias=eps_col.ap[:, 0:1]),
                     reads=[pss, eps_col], writes=[rs])
                k.op(ACT, lambda: nc.scalar.activation(out=rs.ap[:, 0:ntok], in_=rs.ap[:, 0:ntok], func=AF.Exp,
                                                       scale=-0.5), reads=[rs], writes=[rs])
                k.op(DVE, lambda: nc.vector.scalar_tensor_tensor(out=dst_ap, in0=pbank.ap[:, 0:ntok],
                                                                 scalar=gcol.ap[:, 0:1], in1=rs.ap[:, 0:ntok],
                                                                 op0=ALU.mult, op1=ALU.mult),
                     reads=[pbank, gcol, rs], writes=[dstbuf])
            return cont

        pending = []

        def flush_pending():
            while pending:
                pending.pop(0)()

        def proj_F(xT, ntok_sub, W, c0, ncols, each_head, rot=4):
            slot = ws.acquire((W, 0, D, c0, ncols))
            for h in range(ncols // 128):
                for s, (t0, nt_) in enumerate(ntok_sub):
                    pb = banks[f_bank[0] % rot]
                    f_bank[0] += 1

                    def mm(pb=pb, h=h, t0=t0, nt_=nt_):
                        ins = None
                        for kc in range(NCH):
                            ins = nc.tensor.matmul(pb.ap[:, 0:nt_], lhsT=slot.ap[:, kc, h * 128:(h + 1) * 128],
                                                   rhs=xT.ap[:, kc, t0:t0 + nt_], start=(kc == 0),
                                                   stop=(kc == NCH - 1))
                        return ins
                    k.op(PE, mm, reads=[xT, slot], writes=[pb])
                    flush_pending()
                    c_ = each_head(h, s, pb)
                    if c_ is not None:
                        pending.append(c_)

        def proj_T(xT, blocks_local, W, k0, nk, c0, ncols, pbs, first, last, kc_of):
            slot = ws.acquire((W, k0, nk, c0, ncols))

            def mm():
                ins = None
                nkc = nk // 128
                for kc in range(nkc):
                    for bi, tb in enumerate(blocks_local):
                        ins = nc.tensor.matmul(pbs[bi].ap[:, 0:ncols], lhsT=xT.ap[:, kc_of + kc, tb * 128:(tb + 1) * 128],
                                               rhs=slot.ap[:, kc, 0:ncols], start=(first and kc == 0),
                                               stop=(last and kc == nkc - 1))
                return ins
            k.op(PE, mm, reads=[xT, slot], writes=list(pbs))

        f_bank = [0]
        t_bank = [0]

        with contextlib.ExitStack() as ps_:
            TBmax = cfg.P0_TB * 128
            xT = sbt(ps_, [128, NCH, TBmax], BF16, "p0_xT")
            kst = [sbt(ps_, [128, TBmax], BF16, f"p0_kst{i}") for i in range(2)]
            vst = [sbt(ps_, [128, 4, VAW], BF16, f"p0_vst{i}") for i in range(2)]
            sq = [sbt(ps_, [128, 512], BF16, f"p0_sq{i}") for i in range(2)]
            rs = [sbt(ps_, [128, 512], F32, f"p0_rs{i}") for i in range(2)]
            for v_ in vst:
                k.op(DVE, lambda v_=v_: nc.vector.memset(v_.ap[:], 0.0), writes=[v_])
            cnt = [0, 0, 0]
            a_xin = sbt(ps_, [128, D], F32, "p0a_xin")
            a_xn = sbt(ps_, [128, D], BF16, "p0a_xn")
            a_ss = sbt(ps_, [128, 1], F32, "p0a_ss")
            a_rs = sbt(ps_, [128, 1], F32, "p0a_rs")
            a_stage = sbt(ps_, [128, NCH, 512], BF16, "p0a_stage")
            xl_buf = Buf(None, "xloc")
            tiles0 = p0_tiles(cfg)
            xnt_bufs = [Buf(None, f"XNT{i}", accum=True) for i in range(len(tiles0))]

            def p0a_tasks(ti):
                tasks = []
                for st in subtiles(tiles0[ti]):
                    for bi, blk in enumerate(st):
                        for part in (1, 2):
                            tasks.append(lambda bi=bi, blk=blk, part=part: nt_block(
                                xloc[blk * 128:(blk + 1) * 128, :], xl_buf, 0, a_stage,
                                lambda c0, ncg, bi=bi: a_stage.ap[:, c0:c0 + ncg, bi * 128:(bi + 1) * 128],
                                a_xin, a_xn, a_ss, a_rs, part))
                    nt_ = len(st) * 128
                    tasks.append(lambda st=st, nt_=nt_, ti=ti: k.dma(
                        SP, [(XNT[:, :, st[0] * 128:st[0] * 128 + nt_].rearrange("c p t -> p c t"),
                              a_stage.ap[:, :, 0:nt_])], a_stage, reads=[a_stage], writes=[xnt_bufs[ti], dr["XNT"]]))
                return tasks
            bg = []
            bgc = [0]

            def bg_step():
                bgc[0] += 1
                if bg and bgc[0] % 2 == 0:
                    bg.pop(0)()
            for t_ in p0a_tasks(0):
                t_()
            for ti, blocks in enumerate(tiles0):
                while bg:
                    bg.pop(0)()
                if ti + 1 < len(tiles0):
                    bg.extend(p0a_tasks(ti + 1))
                ntok = len(blocks) * 128
                tok0 = blocks[0] * 128
                pairs = []
                for c0 in range(0, NCH, 8):
                    c1 = min(NCH, c0 + 8)
                    pairs.append((xT.ap[:, c0:c1, 0:ntok],
                                  XNT[c0:c1, :, tok0:tok0 + ntok].rearrange("c p t -> p c t")))
                k.dma(SP, pairs, xT, reads=[xnt_bufs[ti]], writes=[xT])
                subs = [(i * 512, min(512, ntok - i * 512)) for i in range((ntok + 511) // 512)]
                for p in range(D // PW):
                    stash = {}

                    def each_k(h, s, pb, p=p, stash=stash):
                        m = p * (PW // 128) + h
                        if s == 0:
                            stash[h] = kst[cnt[0] % 2]
                            cnt[0] += 1
                        kb_ = stash[h]
                        t0, nt_ = subs[s]
                        i2 = cnt[1] % 2
                        cnt[1] += 1
                        c1 = headnorm_evac(pb, nt_, gk_a, kb_, kb_.ap[:, t0:t0 + nt_], sq[i2], banks[4 + i2], rs[i2])

                        def cont(c1=c1, s=s, m=m, kb_=kb_):
                            c1()
                            if s == len(subs) - 1:
                                k.dma(SP, [(KT0[m, :, tok0:tok0 + ntok], kb_.ap[:, 0:ntok])], kb_, reads=[kb_],
                                      writes=[dr["KT0"]])
                        return cont
                    proj_F(xT, subs, w_qkv, D + p * PW, PW, each_k)
                    bg_step()
                flush_pending()
                for p in range(D // PW):
                    first = True
                    for sblocks in subtiles(blocks):
                        if not first:
                            pass
                        first = False
                    slot = ws.acquire((w_qkv, 0, D, 2 * D + p * PW, PW))
                    for sblocks in subtiles(blocks):
                        pbs = [banks[(t_bank[0] + i) % 8] for i in range(len(sblocks))]
                        t_bank[0] += 4

                        def mm(sblocks=sblocks, pbs=pbs, slot=slot):
                            ins = None
                            for kc in range(NCH):
                                for bi, blk in enumerate(sblocks):
                                    tb = blk - blocks[0]
                                    ins = nc.tensor.matmul(pbs[bi].ap[:, 0:PW],
                                                           lhsT=xT.ap[:, kc, tb * 128:(tb + 1) * 128],
                                                           rhs=slot.ap[:, kc, 0:PW], start=(kc == 0),
                                                           stop=(kc == NCH - 1))
                            return ins
                        k.op(PE, mm, reads=[xT, slot], writes=pbs)
                        vs = vst[cnt[2] % 2]
                        cnt[2] += 1
                        for bi, blk in enumerate(sblocks):
                            k.op(ACT, lambda bi=bi, vs=vs, pbs=pbs: nc.scalar.copy(out=vs.ap[:, bi, 0:256],
                                                                                  in_=pbs[bi].ap[:, 0:256]),
                                 reads=[pbs[bi]], writes=[vs])
                        nb_ = len(sblocks)
                        k.op(DVE, lambda vs=vs, sblocks=sblocks, nb_=nb_: nc.vector.tensor_copy(
                            out=vs.ap[:, 0:nb_, 256:257],
                            in_=valid_sb.ap[:, sblocks[0]:sblocks[0] + nb_].unsqueeze(2)),
                            reads=[valid_sb], writes=[vs])
                        k.dma(SP, [(VA0[p, sblocks[0] * 128:(sblocks[-1] + 1) * 128, :].rearrange(
                            "(b q) e -> q b e", q=128), vs.ap[:, 0:nb_, :])], vs, reads=[vs], writes=[dr["VA0"]])
                    bg_step()
                if any(b >= A0 for b in blocks):
                    qsubs = []
                    for i, (t0, nt_) in enumerate(subs):
                        sb_blocks = blocks[i * 4:i * 4 + 4]
                        if any(b >= A0 for b in sb_blocks):
                            qsubs.append((t0, nt_, sb_blocks))
                    for p in range(D // PW):
                        def each_q(h, s, pb, p=p):
                            m = p * (PW // 128) + h
                            t0, nt_, sb_blocks = qsubs[s]
                            qb_ = kst[cnt[0] % 2]
                            cnt[0] += 1
                            i2 = cnt[1] % 2
                            cnt[1] += 1
                            c1 = headnorm_evac(pb, nt_, gq_a, qb_, qb_.ap[:, 0:nt_], sq[i2], banks[4 + i2], rs[i2])
                            act_b = [b for b in sb_blocks if b >= A0]
                            o0 = (act_b[0] - sb_blocks[0]) * 128
                            n_ = len(act_b) * 128
                            a0 = act_b[0] - A0

                            def cont(c1=c1, m=m, a0=a0, n_=n_, o0=o0, qb_=qb_):
                                c1()
                                k.dma(SP, [(QT0[m, :, a0 * 128:a0 * 128 + n_], qb_.ap[:, o0:o0 + n_])], qb_,
                                      reads=[qb_], writes=[dr["QT0"]])
                            return cont
                        proj_F(xT, [(t0, nt_) for (t0, nt_, _) in qsubs], w_qkv, p * PW, PW, each_q)
                        bg_step()
                    flush_pending()
            while bg:
                bg.pop(0)()
            k.barrier()

        if debug_phase >= 1:
            phase1(locals())
        if debug_phase >= 2:
            phase23(locals())

        k.barrier()
        nc._k_stats = {e.name: e.cnt for e in k.engs}
        nc._k_stats["max_dma_sem"] = max(16 * b.dcnt for b in k.dbufs)
        nc._k_stats["nsems"] = len(k.sems)
    return nc


def phase1(L):
    g = L
    nc, k, cfg = g["nc"], g["k"], g["cfg"]
    PE, ACT, DVE, POOL, SP = k.PE, k.ACT, k.DVE, k.POOL, k.SP
    banks, dr, sbt, bank2 = g["banks"], g["dr"], g["sbt"], g["bank2"]
    KT0, VA0, QT0, O0 = g["KT0"], g["VA0"], g["QT0"], g["O0"]
    NB, NA, H0, A0 = cfg.NB, cfg.NA, cfg.H0, g["A0"]
    cfar, neglam, load_eb, atiles, scale = g["cfar"], g["neglam"], g["load_eb"], g["atiles"], g["scale"]
    with contextlib.ExitStack() as ps_:
        kt2 = [[sbt(ps_, [128, NB * 128], BF16, f"p1_kt{i}{mi}") for mi in range(2)] for i in range(2)]
        va = sbt(ps_, [128, NB, VAW], BF16, "p1_va")
        qt2 = [[sbt(ps_, [128, NA * 128], BF16, f"p1_qt{i}{mi}") for mi in range(2)] for i in range(2)]
        eb2 = [sbt(ps_, [128, 2, 2, 128], F32, f"p1_eb{i}") for i in range(2)]
        pt = [sbt(ps_, [128, 2, 512], BF16, f"p1_pt{i}") for i in range(4)]
        ptmp = [sbt(ps_, [128, 128], F32, f"p1_ptmp{i}") for i in range(2)]
        o1 = [sbt(ps_, [128, 256], F32, f"p1_o1{i}") for i in range(4)]
        t2 = [sbt(ps_, [128, 256], F32, f"p1_t2{i}") for i in range(2)]
        rz = [sbt(ps_, [128, 1], F32, f"p1_rz{i}") for i in range(4)]
        ost = [sbt(ps_, [128, 4, 256], BF16, f"p1_ost{i}") for i in range(2)]
        cn = {"s": 0, "pt": 0, "tmp": 0, "rz": 0, "t2": 0, "ost": 0}

        def load_kq(h):
            i = h % 2
            for mi in range(2):
                m = 2 * h + mi
                k.dma(SP, [(kt2[i][mi].ap[:], KT0[m])], kt2[i][mi], reads=[dr["KT0"]], writes=[kt2[i][mi]])
                k.dma(SP, [(qt2[i][mi].ap[:], QT0[m])], qt2[i][mi], reads=[dr["QT0"]], writes=[qt2[i][mi]])
            with nc.allow_non_contiguous_dma(reason="toeplitz bias tiles"):
                k.dma(SP, [load_eb(eb2[i].ap[:, 0, :, :], 0, 2 * h, 2), load_eb(eb2[i].ap[:, 1, :, :], 1, 2 * h, 2)],
                      eb2[i], reads=[dr["GX"]], writes=[eb2[i]])

        load_kq(0)
        for h in range(H0):
            kt, qt, eb = kt2[h % 2], qt2[h % 2], eb2[h % 2]
            k.dma(SP, [(va.ap[:], VA0[h].rearrange("(b q) e -> q b e", q=128))], va, reads=[dr["VA0"]], writes=[va])
            if h + 1 < H0:
                load_kq(h + 1)
            for tl in atiles:
                nq = len(tl)
                gq = [A0 + a for a in tl]
                q0 = tl[0] * 128
                ostg = ost[cn["ost"] % 2]
                cn["ost"] += 1
                for mi in range(2):
                    m = 2 * h + mi
                    obanks = [banks[4 + j] for j in range(nq)]
                    def emit_S(kb, mi=mi, m=m):
                        pair = (kb + 1 <= gq[0] - 2)
                        ps_i = cn["s"] % 2
                        cn["s"] += 1
                        ptb = pt[cn["pt"] % 4]
                        cn["pt"] += 1
                        nkb = 2 if pair else 1
                        sb_ = [banks[2 * ps_i + t_] for t_ in range(nkb)]

                        def mmS(kb=kb, mi=mi, sb_=sb_, nkb=nkb):
                            ins = None
                            for t_ in range(nkb):
                                ins = nc.tensor.matmul(sb_[t_].ap[:, 0:nq * 128],
                                                       lhsT=kt[mi].ap[:, (kb + t_) * 128:(kb + t_ + 1) * 128],
                                                       rhs=qt[mi].ap[:, q0:q0 + nq * 128], start=True, stop=True)
                            return ins
                        k.op(PE, mmS, reads=[kt[mi], qt[mi]], writes=sb_)
                        if pair:
                            k.op(ACT, lambda ptb=ptb, ps_i=ps_i, m=m: nc.scalar.activation(
                                out=ptb.ap[:, :, 0:nq * 128], in_=bank2[ps_i][:, :, 0:nq * 128], func=AF.Exp,
                                bias=cfar.ap[:, m:m + 1], scale=scale), reads=sb_ + [cfar], writes=[ptb])
                            return (2, ptb)
                        sbk = sb_[0]
                        far = [j for j in range(nq) if gq[j] >= kb + 2]
                        near = [j for j in range(nq) if gq[j] in (kb, kb + 1)]
                        if far:
                            c0_, c1_ = far[0] * 128, (far[-1] + 1) * 128
                            k.op(ACT, lambda ptb=ptb, sbk=sbk, c0_=c0_, c1_=c1_, m=m: nc.scalar.activation(
                                out=ptb.ap[:, 0, c0_:c1_], in_=sbk.ap[:, c0_:c1_], func=AF.Exp,
                                bias=cfar.ap[:, m:m + 1], scale=scale), reads=[sbk, cfar], writes=[ptb])
                        for j in near:
                            kind = 0 if gq[j] == kb else 1
                            tmp = ptmp[cn["tmp"] % 2]
                            cn["tmp"] += 1
                            k.op(ACT, lambda tmp=tmp, sbk=sbk, j=j: nc.scalar.activation(
                                out=tmp.ap[:], in_=sbk.ap[:, j * 128:(j + 1) * 128], func=AF.Exp, scale=scale),
                                reads=[sbk], writes=[tmp])
                            k.op(DVE, lambda tmp=tmp, ptb=ptb, j=j, kind=kind, mi=mi: nc.vector.tensor_tensor(
                                out=ptb.ap[:, 0, j * 128:(j + 1) * 128], in0=tmp.ap[:], in1=eb.ap[:, kind, mi, :],
                                op=ALU.mult), reads=[tmp, eb], writes=[ptb])
                        return (1, ptb)

                    def emit_PV(kb, nkb, ptb):
                        def pv(kb=kb, nkb=nkb, ptb=ptb):
                            ins = None
                            for t_ in range(nkb):
                                kk = kb + t_
                                for j in range(nq):
                                    if gq[j] < kk:
                                        continue
                                    ins = nc.tensor.matmul(obanks[j].ap[:, 0:257],
                                                           lhsT=ptb.ap[:, t_, j * 128:(j + 1) * 128],
                                                           rhs=va.ap[:, kk, 0:257], start=(kk == 0),
                                                           stop=(kk == gq[j]))
                            return ins
                        js = [j for j in range(nq) if gq[j] >= kb]
                        k.op(PE, pv, reads=[ptb, va], writes=[obanks[j] for j in js])

                    def evac(j, mi=mi):
                        r_ = rz[cn["rz"] % 4]
                        cn["rz"] += 1
                        k.op(DVE, lambda r_=r_, j=j: nc.vector.tensor_scalar(
                            out=r_.ap[:], in0=obanks[j].ap[:, 256:257], scalar1=1e-30, scalar2=None, op0=ALU.max),
                            reads=[obanks[j]], writes=[r_])
                        k.op(DVE, lambda r_=r_: nc.vector.reciprocal(out=r_.ap[:], in_=r_.ap[:]), reads=[r_],
                             writes=[r_])
                        if mi == 0:
                            k.op(ACT, lambda r_=r_, j=j: nc.scalar.activation(
                                out=o1[j].ap[:], in_=obanks[j].ap[:, 0:256], func=AF.Copy, scale=r_.ap[:, 0:1]),
                                reads=[obanks[j], r_], writes=[o1[j]])
                        else:
                            t_ = t2[cn["t2"] % 2]
                            cn["t2"] += 1
                            k.op(DVE, lambda r_=r_, j=j, t_=t_: nc.vector.tensor_scalar(
                                out=t_.ap[:], in0=obanks[j].ap[:, 0:256], scalar1=r_.ap[:, 0:1],
                                scalar2=neglam.ap[:, 0:1], op0=ALU.mult, op1=ALU.mult),
                                reads=[obanks[j], r_, neglam], writes=[t_])
                            k.op(DVE, lambda j=j, t_=t_, ostg=ostg: nc.vector.tensor_tensor(
                                out=ostg.ap[:, j, :], in0=o1[j].ap[:], in1=t_.ap[:], op=ALU.add),
                                reads=[o1[j], t_], writes=[ostg])

                    def pv_and_evac(kb_, nkb_, ptb_):
                        emit_PV(kb_, nkb_, ptb_)
                        for j in range(nq):
                            if kb_ <= gq[j] < kb_ + nkb_:
                                evac(j)

                    inflight = []
                    kb = 0
                    while kb <= gq[-1]:
                        nkb, ptb = emit_S(kb)
                        inflight.append((kb, nkb, ptb))
                        kb += nkb
                        if len(inflight) > 1:
                            pv_and_evac(*inflight.pop(0))
                    while inflight:
                        pv_and_evac(*inflight.pop(0))
                k.dma(SP, [(O0[q0:q0 + nq * 128, h * 256:(h + 1) * 256].rearrange("(b q) e -> q b e", q=128),
                            ostg.ap[:, 0:nq, :])], ostg, reads=[ostg], writes=[dr["O0"]])
        k.barrier()


def phase23(L):
    import os
    P3STOP = int(os.environ.get("K_P3_STOP", "0"))
    g = L
    nc, k, cfg, ws = g["nc"], g["k"], g["cfg"], g["ws"]
    PE, ACT, DVE, POOL, SP = k.PE, k.ACT, k.DVE, k.POOL, k.SP
    banks, dr, sbt, bank_bf16 = g["banks"], g["dr"], g["sbt"], g["bank_bf16"]
    D, DFF, NB, NQ, NA, NCH, FCH = cfg.D, cfg.DFF, cfg.NB, cfg.NQ, cfg.NA, cfg.NCH, cfg.FCH
    QH, KVH, H0, A0, KVW = cfg.QH, cfg.KVH, cfg.H0, g["A0"], g["KVW"]
    ident, ones, gcols, gsub, valid_sb, esink = g["ident"], g["ones"], g["gcols"], g["gsub"], g["valid_sb"], g["esink"]
    gk_s, gq_b = g["gk_s"], g["gq_b"]
    eps_col, lnli_col = g["eps_col"], g["lnli_col"]
    xloc, O0, X1S, X2, KSHT, VSH, X3S, out = g["xloc"], g["O0"], g["X1S"], g["X2"], g["KSHT"], g["VSH"], g["X3S"], g["out"]
    w_o_a, w_o_b, w_q_b, w_kv, w_gate, w_up, w_down = (g["w_o_a"], g["w_o_b"], g["w_q_b"], g["w_kv"], g["w_gate"],
                                                        g["w_up"], g["w_down"])
    load_eb, atiles, scale, debug_phase = g["load_eb"], g["atiles"], g["scale"], g["debug_phase"]
    lambda_init0 = g["lambda_init0"]
    headnorm_evac, proj_F, proj_T = g["headnorm_evac"], g["proj_F"], g["proj_T"]
    flush_pending = g["flush_pending"]
    f_bank, t_bank, nt_bank = g["f_bank"], g["t_bank"], g["nt_bank"]

    with contextlib.ExitStack() as ps_:
        xT = sbt(ps_, [128, NCH, 512], BF16, "p2_xT")
        act_bytes = FCH * 512 * 2
        lay = {}
        off = 0

        def place(name, shape, dt, at=None):
            nonlocal off
            nb_ = int(np.prod(shape[1:])) * (4 if dt == F32 else 2)
            lo = off if at is None else at
            lay[name] = (lo, lo + nb_, shape, dt)
            if at is None:
                off = lo + nb_
            return lo + nb_
        place("xin", [128, D], F32)
        place("xn", [128, D], BF16)
        e1 = off
        place("ob0", [128, D], BF16)
        place("ob1", [128, D], BF16)
        place("otmp", [128, D], F32)
        place("xin2", [128, D], F32)
        e2 = off
        off = 0
        place("qT", [128, QH, 512], BF16)
        off = max(off, e1)
        place("ksh", [128, KVH, 5 * 128], BF16)
        place("vsh", [128, 5, KVH, VSW], BF16)
        place("ebs", [128, 2, QH, 128], BF16)
        place("oq0", [128, D], BF16)
        place("oq1", [128, D], BF16)
        ubytes = max(act_bytes, e2, off)
        ubytes = (ubytes + 3) // 4 * 4
        U = ps_.enter_context(nc.sbuf_tensor("p23_union", [128, ubytes // 2], BF16))
        members = []
        actT = Buf(U[:, 0:act_bytes // 2].rearrange("p (f t) -> p f t", t=512), "actT", union=members, lo=0,
                   hi=act_bytes)

        def carve(name):
            lo, hi, shape, dt = lay[name]
            flat = U[:, lo // 2:hi // 2]
            if dt == F32:
                flat = flat.bitcast(F32)
            if len(shape) == 2:
                ap = flat
            elif len(shape) == 3:
                ap = flat.rearrange("p (a b) -> p a b", b=shape[2])
            else:
                ap = flat.rearrange("p (a b c) -> p a b c", b=shape[2], c=shape[3])
            return Buf(ap, name, union=members, lo=lo, hi=hi)
        nt_xin = [carve("xin"), carve("xin2")]
        nt_xn = [carve("xn")]
        ob = [carve("ob0"), carve("ob1")]
        otmp = carve("otmp")
        qT = carve("qT")
        ksh = carve("ksh")
        vsh = carve("vsh")
        ebs = carve("ebs")
        oq = [carve("oq0"), carve("oq1")]
        nt_ss = sbt(ps_, [128, 1], F32, "p2_ss")
        nt_rs = sbt(ps_, [128, 1], F32, "p2_rs")
        rst = [sbt(ps_, [128, 4, PW], F32, f"p2_rst{i}") for i in range(1)] * 2
        ost = [sbt(ps_, [128, 4, PW], F32, f"p2_ost{i}") for i in range(1)] * 2
        sg = [sbt(ps_, [128, 512], F32, f"p2_sg{i}") for i in range(1)] * 2
        sq = [sbt(ps_, [128, 512], BF16, f"p2_sq{i}") for i in range(2)]
        rs = [sbt(ps_, [128, 512], F32, f"p2_rs{i}") for i in range(1)] * 2
        kst = [sbt(ps_, [128, 512], BF16, f"p2_kst{i}") for i in range(2)]
        vst = [sbt(ps_, [128, 4, 2, VSW], BF16, f"p2_vst{i}") for i in range(2)]
        hrs = sbt(ps_, [128, H0], F32, "p2_hrs")
        pts = [sbt(ps_, [128, 4, 128], BF16, f"p3_pt{i}") for i in range(4)]
        ptmp = [sbt(ps_, [128, 4, 128], F32, f"p3_ptmp{i}") for i in range(2)]
        zt = [sbt(ps_, [128, 4], F32, f"p3_zt{i}") for i in range(2)]
        for v_ in vst:
            k.op(DVE, lambda v_=v_: nc.vector.memset(v_.ap[:], 0.0), writes=[v_])
        cn = {"rst": 0, "ost": 0, "sg": 0, "sq": 0, "kst": 0, "vst": 0}

        def norm_T(src, src_buf, nblk, gidx):
            for b in range(nblk):
                xin = nt_xin[b % 2]
                xn = nt_xn[0]
                k.dma(SP, [(xin.ap[:], src[b * 128:(b + 1) * 128, :])], xin, reads=[src_buf], writes=[xin])
                k.op(ACT, lambda xin=xin, xn=xn: nc.scalar.activation(out=xn.ap[:], in_=xin.ap[:], func=AF.Square,
                                                                      accum_out=nt_ss.ap[:]),
                     reads=[xin], writes=[xn, nt_ss])
                k.op(ACT, lambda: nc.scalar.activation(out=nt_rs.ap[:], in_=nt_ss.ap[:], func=AF.Ln, scale=1.0 / D,
                                                       bias=eps_col.ap[:, 0:1]), reads=[nt_ss, eps_col], writes=[nt_rs])
                k.op(ACT, lambda: nc.scalar.activation(out=nt_rs.ap[:], in_=nt_rs.ap[:], func=AF.Exp, scale=-0.5),
                     reads=[nt_rs], writes=[nt_rs])
                k.op(DVE, lambda xin=xin, xn=xn: nc.vector.tensor_scalar(out=xn.ap[:], in0=xin.ap[:],
                                                                         scalar1=nt_rs.ap[:, 0:1], scalar2=None,
                                                                         op0=ALU.mult),
                     reads=[xin, nt_rs], writes=[xn])
                transpose_block(xn, b, gidx)

        def transpose_block(xn, b, gidx):
            for grp in range((NCH + 7) // 8):
                c0 = grp * 8
                ncg = min(8, NCH - c0)
                pb = banks[nt_bank[0] % 2 + 6]
                nt_bank[0] += 1
                pv = bank_bf16(pb)[:, 0:ncg * 128].rearrange("p (c t) -> p c t", c=ncg)

                def tr(pv=pv, c0=c0, ncg=ncg, xn=xn):
                    ins = None
                    for c in range(ncg):
                        ins = nc.tensor.transpose(pv[:, c, :], xn.ap[:, (c0 + c) * 128:(c0 + c + 1) * 128],
                                                  ident.ap[:])
                    return ins
                k.op(PE, tr, reads=[xn, ident], writes=[pb])
                dst = xT.ap[:, c0:c0 + ncg, b * 128:(b + 1) * 128]
                if gidx is None:
                    k.op(ACT, lambda dst=dst, pv=pv: nc.scalar.copy(out=dst, in_=pv), reads=[pb], writes=[xT])
                else:
                    gb = gcols.ap[:, gidx, c0:c0 + ncg].unsqueeze(2).to_broadcast([128, ncg, 128])
                    k.op(DVE, lambda dst=dst, pv=pv, gb=gb: nc.vector.tensor_tensor(out=dst, in0=pv, in1=gb,
                                                                                   op=ALU.mult),
                         reads=[pb, gcols], writes=[xT])

        def resid_proj(W, nblk, Kin, inT, res_src, res_buf, dst, dst_buf, kch):
            for p in range(D // PW):
                pbs = [banks[(t_bank[0] + i_) % 8] for i_ in range(nblk)]
                t_bank[0] += 4
                k0 = 0
                while k0 < Kin:
                    nk = min(32 * 128, Kin - k0)
                    proj_T(inT, list(range(nblk)), W, k0, nk, p * PW, PW, pbs, k0 == 0, k0 + nk == Kin, k0 // 128)
                    k0 += nk
                ri = rst[cn["rst"] % 2]
                cn["rst"] += 1
                oi = ost[cn["ost"] % 2]
                cn["ost"] += 1
                k.dma(SP, [(ri.ap[:, 0:nblk, :], res_src[:, p * PW:(p + 1) * PW].rearrange("(b q) e -> q b e", q=128))],
                      ri, reads=[res_buf], writes=[ri])
                for b in range(nblk):
                    k.op(DVE, lambda b=b, oi=oi, ri=ri, pbs=pbs: nc.vector.tensor_tensor(
                        out=oi.ap[:, b, :], in0=pbs[b].ap[:, 0:PW], in1=ri.ap[:, b, :], op=ALU.add),
                        reads=[pbs[b], ri], writes=[oi])
                k.dma(SP, [(dst[:, p * PW:(p + 1) * PW].rearrange("(b q) e -> q b e", q=128), oi.ap[:, 0:nblk, :])],
                      oi, reads=[oi], writes=[dst_buf])

        def ffn(layer, nblk, res_src, res_buf, dst, dst_buf):
            ntok = nblk * 128
            for p in range(DFF // PW):
                gb_ = []
                for wi, W in enumerate((w_gate[layer], w_up[layer])):
                    got = []
                    proj_F(xT, [(0, ntok)], W, p * PW, PW, lambda h, s, pb, got=got: got.append(pb), rot=8)
                    gb_.append(got)
                for h in range(PW // 128):
                    s_ = sg[cn["sg"] % 2]
                    cn["sg"] += 1
                    pg, pu = gb_[0][h], gb_[1][h]
                    k.op(ACT, lambda s_=s_, pg=pg: nc.scalar.activation(out=s_.ap[:, 0:ntok], in_=pg.ap[:, 0:ntok],
                                                                        func=AF.Silu), reads=[pg], writes=[s_])
                    fc = p * (PW // 128) + h
                    k.op(DVE, lambda s_=s_, pu=pu, fc=fc: nc.vector.tensor_tensor(
                        out=actT.ap[:, fc, 0:ntok], in0=pu.ap[:, 0:ntok], in1=s_.ap[:, 0:ntok], op=ALU.mult),
                        reads=[pu, s_], writes=[actT])
            resid_proj(w_down[layer], nblk, DFF, actT, res_src, res_buf, dst, dst_buf, FCH)

        with contextlib.ExitStack() as p2s:
            for tl in atiles:
                nblk = len(tl)
                r0, r1 = tl[0] * 128, (tl[-1] + 1) * 128
                for b in range(nblk):
                    o_ = ob[b % 2]
                    k.dma(SP, [(o_.ap[:], O0[r0 + b * 128:r0 + (b + 1) * 128, :])], o_, reads=[dr["O0"]], writes=[o_])
                    k.op(DVE, lambda o_=o_: nc.vector.tensor_tensor(out=otmp.ap[:], in0=o_.ap[:], in1=o_.ap[:],
                                                                    op=ALU.mult), reads=[o_], writes=[otmp])
                    k.op(DVE, lambda: nc.vector.tensor_reduce(out=hrs.ap[:], in_=otmp.ap[:].rearrange(
                        "p (h e) -> p h e", e=256), axis=AX.X, op=ALU.add), reads=[otmp], writes=[hrs])
                    k.op(ACT, lambda: nc.scalar.activation(out=hrs.ap[:], in_=hrs.ap[:], func=AF.Ln, scale=1.0 / 256,
                                                           bias=eps_col.ap[:, 0:1]), reads=[hrs, eps_col], writes=[hrs])
                    k.op(ACT, lambda: nc.scalar.activation(out=hrs.ap[:], in_=hrs.ap[:], func=AF.Exp, scale=-0.5,
                                                           bias=lnli_col.ap[:, 0:1]), reads=[hrs, lnli_col],
                         writes=[hrs])
                    k.op(DVE, lambda o_=o_: nc.vector.tensor_tensor(
                        out=otmp.ap[:].rearrange("p (h e) -> p h e", e=256),
                        in0=o_.ap[:].rearrange("p (h e) -> p h e", e=256),
                        in1=hrs.ap[:].unsqueeze(2).to_broadcast([128, H0, 256]), op=ALU.mult),
                        reads=[o_, hrs], writes=[otmp])
                    k.op(DVE, lambda o_=o_: nc.vector.tensor_tensor(
                        out=o_.ap[:].rearrange("p (h e) -> p h e", e=256),
                        in0=otmp.ap[:].rearrange("p (h e) -> p h e", e=256),
                        in1=gsub.ap[:].unsqueeze(1).to_broadcast([128, H0, 256]), op=ALU.mult),
                        reads=[otmp, gsub], writes=[o_])
                    transpose_block(o_, b, None)
                resid_proj(w_o_a, nblk, D, xT, xloc[A0 * 128 + r0:A0 * 128 + r1, :], Buf(None, "xl"),
                           X1S[r0:r1, :], dr["X1S"], NCH)
                norm_T(X1S[r0:r1, :], dr["X1S"], nblk, 2)
                ffn(0, nblk, X1S[r0:r1, :], dr["X1S"], X2[r0:r1, :], dr["X2"])
                norm_T(X2[r0:r1, :], dr["X2"], nblk, 4)
                wkv_pw = min(PW, KVW)
                for p in range(KVW // wkv_pw):
                    def each_k(h, s, pb, p=p):
                        gh = p * (wkv_pw // 128) + h
                        kb_ = kst[cn["kst"] % 2]
                        cn["kst"] += 1
                        i2 = cn["sq"] % 2
                        cn["sq"] += 1
                        c1 = headnorm_evac(pb, nblk * 128, gk_s, kb_, kb_.ap[:, 0:nblk * 128], sq[i2], banks[4 + i2],
                                           rs[i2])

                        def cont(c1=c1, gh=gh, kb_=kb_):
                            c1()
                            k.dma(SP, [(KSHT[gh, :, r0:r1], kb_.ap[:, 0:nblk * 128])], kb_, reads=[kb_],
                                  writes=[dr["KSHT"]])
                        return cont
                    proj_F(xT, [(0, nblk * 128)], w_kv, p * wkv_pw, wkv_pw, each_k)
                flush_pending()
                for p in range(KVW // wkv_pw):
                    pbs = [banks[(t_bank[0] + i_) % 4] for i_ in range(nblk)]
                    t_bank[0] += 4
                    proj_T(xT, list(range(nblk)), w_kv, 0, D, KVW + p * wkv_pw, wkv_pw, pbs, True, True, 0)
                    vs = vst[cn["vst"] % 2]
                    cn["vst"] += 1
                    nh_ = wkv_pw // 128
                    for b in range(nblk):
                        k.op(ACT, lambda b=b, vs=vs, pbs=pbs: nc.scalar.copy(
                            out=vs.ap[:, b, 0:nh_, 0:128],
                            in_=pbs[b].ap[:, 0:wkv_pw].rearrange("q (h e) -> q h e", e=128)),
                            reads=[pbs[b]], writes=[vs])
                    for hh in range(nh_):
                        k.op(DVE, lambda vs=vs, hh=hh: nc.vector.tensor_copy(
                            out=vs.ap[:, 0:nblk, hh, 128:129],
                            in_=valid_sb.ap[:, A0 + tl[0]:A0 + tl[0] + nblk].unsqueeze(2)),
                            reads=[valid_sb], writes=[vs])
                    pairs = []
                    for hh in range(nh_):
                        gh = p * nh_ + hh
                        pairs.append((VSH[gh, r0:r1, :].rearrange("(b q) e -> q b e", q=128), vs.ap[:, 0:nblk, hh, :]))
                    k.dma(SP, pairs, vs, reads=[vs], writes=[dr["VSH"]])
            k.barrier()

        if debug_phase < 3:
            return
        with contextlib.ExitStack() as p3s:
            c3 = {"pt": 0, "zt": 0, "s": 0}
            own_tiles = [list(range(a, min(a + 4, NQ))) for a in range(0, NQ, 4)]
            for tl in own_tiles:
                nblk = len(tl)
                a0 = 1 + tl[0]
                r0, r1 = a0 * 128, (a0 + nblk) * 128
                norm_T(X2[r0:r1, :], dr["X2"], nblk, 1)
                for p in range(D // PW):
                    def each_q(h, s, pb, p=p):
                        qh = p * (PW // 128) + h
                        i2 = cn["sq"] % 2
                        cn["sq"] += 1
                        return headnorm_evac(pb, nblk * 128, gq_b, qT, qT.ap[:, qh, 0:nblk * 128], sq[i2],
                                             banks[4 + i2], rs[i2])
                    proj_F(xT, [(0, nblk * 128)], w_q_b, p * PW, PW, each_q)
                flush_pending()
                if P3STOP == 2:
                    continue
                with nc.allow_non_contiguous_dma(reason="toeplitz bias tiles"):
                    k.dma(SP, [load_eb(ebs.ap[:, 0, :, :], 2, 0, QH, True), load_eb(ebs.ap[:, 1, :, :], 0, 0, QH, True)],
                          ebs, reads=[dr["GX"]], writes=[ebs])
                k.dma(SP, [(ksh.ap[:, :, 0:(nblk + 1) * 128], KSHT[:, :, r0 - 128:r1].rearrange("g d t -> d g t"))],
                      ksh, reads=[dr["KSHT"]], writes=[ksh])
                pairs = []
                for gh in range(KVH):
                    pairs.append((vsh.ap[:, 0:nblk + 1, gh, :],
                                  VSH[gh, r0 - 128:r1, :].rearrange("(b q) e -> q b e", q=128)))
                k.dma(SP, pairs, vsh, reads=[dr["VSH"]], writes=[vsh])
                for b in range(nblk):
                    oq_ = oq[b % 2]
                    for gh in range(KVH):
                        ob2 = [banks[4 + 2 * (c3["s"] % 2)], banks[5 + 2 * (c3["s"] % 2)]]
                        ptj = []
                        for j in range(2):
                            sbk = banks[c3["pt"] % 4]
                            kcol = (b + j) * 128
                            k.op(PE, lambda sbk=sbk, kcol=kcol, gh=gh, b=b: nc.tensor.matmul(
                                sbk.ap[:, 0:512].rearrange("k (r q) -> k r q", r=4),
                                lhsT=ksh.ap[:, gh, kcol:kcol + 128],
                                rhs=qT.ap[:, gh * 4:(gh + 1) * 4, b * 128:(b + 1) * 128], start=True, stop=True),
                                reads=[ksh, qT], writes=[sbk])
                            tmp = ptmp[c3["pt"] % 2]
                            ptb = pts[c3["pt"] % 4]
                            c3["pt"] += 1
                            k.op(ACT, lambda tmp=tmp, sbk=sbk: nc.scalar.activation(
                                out=tmp.ap[:], in_=sbk.ap[:, 0:512].rearrange("k (r q) -> k r q", r=4), func=AF.Exp,
                                scale=scale), reads=[sbk], writes=[tmp])
                            k.op(DVE, lambda tmp=tmp, ptb=ptb, j=j, gh=gh: nc.vector.tensor_tensor(
                                out=ptb.ap[:], in0=tmp.ap[:], in1=ebs.ap[:, j, gh * 4:(gh + 1) * 4, :], op=ALU.mult),
                                reads=[tmp, ebs], writes=[ptb])
                            ptj.append(ptb)
                        for half in range(2):
                            def pv(half=half, ptj=ptj, gh=gh, b=b, ob2=ob2):
                                ins = None
                                for rr in range(2):
                                    r = half * 2 + rr
                                    for j in range(2):
                                        ins = nc.tensor.matmul(ob2[half].ap[:, rr * 129:rr * 129 + 129],
                                                               lhsT=ptj[j].ap[:, r, :], rhs=vsh.ap[:, b + j, gh, 0:129],
                                                               start=(j == 0), stop=(j == 1))
                                return ins
                            k.op(PE, pv, reads=[ptj[0], ptj[1], vsh], writes=[ob2[half]])
                        c3["s"] += 1
                        z_ = zt[c3["zt"] % 2]
                        c3["zt"] += 1
                        for half in range(2):
                            k.op(DVE, lambda z_=z_, half=half, gh=gh, ob2=ob2: nc.vector.tensor_tensor(
                                out=z_.ap[:, half * 2:half * 2 + 2],
                                in0=ob2[half].ap[:, 0:258].rearrange("q (r e) -> q r e", e=129)[:, :, 128],
                                in1=esink.ap[:, gh * 4 + half * 2:gh * 4 + half * 2 + 2], op=ALU.add),
                                reads=[ob2[half], esink, z_], writes=[z_])
                        k.op(DVE, lambda z_=z_: nc.vector.reciprocal(out=z_.ap[:], in_=z_.ap[:]), reads=[z_],
                             writes=[z_])
                        for half in range(2):
                            h0_ = gh * 4 + half * 2
                            k.op(DVE, lambda z_=z_, half=half, h0_=h0_, ob2=ob2, oq_=oq_: nc.vector.tensor_tensor(
                                out=oq_.ap[:, h0_ * 128:(h0_ + 2) * 128].rearrange("q (r e) -> q r e", e=128),
                                in0=ob2[half].ap[:, 0:258].rearrange("q (r e) -> q r e", e=129)[:, :, 0:128],
                                in1=z_.ap[:, half * 2:half * 2 + 2].unsqueeze(2).to_broadcast([128, 2, 128]),
                                op=ALU.mult), reads=[ob2[half], z_, oq_], writes=[oq_])
                    transpose_block(oq_, b, None)
                if P3STOP == 3:
                    continue
                resid_proj(w_o_b, nblk, D, xT, X2[r0:r1, :], dr["X2"], X3S[r0 - 128:r1 - 128, :], dr["X3S"], NCH)
                if P3STOP == 4:
                    continue
                norm_T(X3S[r0 - 128:r1 - 128, :], dr["X3S"], nblk, 3)
                ffn(1, nblk, X3S[r0 - 128:r1 - 128, :], dr["X3S"], out[r0 - 128:r1 - 128, :], dr["out"])
            k.barrier()


def make_in_maps(cfg, inputs, n_cores=8):
    x = np.asarray(inputs["x"], dtype=np.float32)
    B, S, D = x.shape
    NB, NQ = cfg.NB, cfg.NQ
    assert S == 4 * NQ * 128 and B * 4 == n_cores
    shared = {
        "rel_bias": np.ascontiguousarray(inputs["rel_bias"], dtype=np.float32),
        "g_attn_norm": np.ascontiguousarray(inputs["g_attn_norm"], dtype=np.float32),
        "g_ffn_norm": np.ascontiguousarray(inputs["g_ffn_norm"], dtype=np.float32),
        "w_qkv_a": np.ascontiguousarray(inputs["w_qkv_a"][0], dtype=np.float32),
        "w_o_a": np.ascontiguousarray(inputs["w_o_a"][0], dtype=np.float32),
        "g_q_a": np.ascontiguousarray(inputs["g_q_a"][0].reshape(128, 1), dtype=np.float32),
        "g_k_a": np.ascontiguousarray(inputs["g_k_a"][0].reshape(128, 1), dtype=np.float32),
        "lam_qk_a": np.ascontiguousarray(inputs["lam_qk_a"][0].reshape(1, 512), dtype=np.float32),
        "g_sub_a": np.ascontiguousarray(inputs["g_sub_a"][0].reshape(1, 256), dtype=np.float32),
        "g_kv_norm": np.ascontiguousarray(inputs["g_kv_norm"].reshape(1, D), dtype=np.float32),
        "w_kv": np.ascontiguousarray(inputs["w_kv"], dtype=np.float32),
        "g_k_shared": np.ascontiguousarray(inputs["g_k_shared"].reshape(128, 1), dtype=np.float32),
        "w_q_b": np.ascontiguousarray(inputs["w_q_b"][0], dtype=np.float32),
        "w_o_b": np.ascontiguousarray(inputs["w_o_b"][0], dtype=np.float32),
        "g_q_b": np.ascontiguousarray(inputs["g_q_b"][0].reshape(128, 1), dtype=np.float32),
        "sinks_b": np.ascontiguousarray(inputs["sinks_b"][0].reshape(1, -1), dtype=np.float32),
        "w_gate": np.ascontiguousarray(inputs["w_gate"], dtype=np.float32),
        "w_up": np.ascontiguousarray(inputs["w_up"], dtype=np.float32),
        "w_down": np.ascontiguousarray(inputs["w_down"], dtype=np.float32),
    }
    maps = []
    for c in range(n_cores):
        b, j = c // 4, c % 4
        n_real = (j + 1) * NQ
        xl = np.zeros((NB * 128, D), np.float32)
        xl[(NB - n_real) * 128:, :] = x[b, :n_real * 128, :]
        v = np.zeros((128, NB), np.float32)
        v[:, NB - n_real:] = 1.0
        m = dict(shared)
        m["xloc"] = xl
        m["valid"] = v
        maps.append(m)
    return maps


def gather_out(cfg, results, B, S, D):
    y = np.zeros((B, S, D), np.float32)
    NQ = cfg.NQ
    for c, r in enumerate(results):
        b, j = c // 4, c % 4
        y[b, j * NQ * 128:(j + 1) * NQ * 128, :] = r["out"]
    return y


def kernel(**inputs):
    cfg = Cfg()
    nc = build_program(cfg)
    maps = make_in_maps(cfg, inputs)
    res = run_bass_kernel_spmd(nc, maps, core_ids=list(range(8)))
    x = inputs["x"]
    return gather_out(cfg, res.results, x.shape[0], x.shape[1], x.shape[2])
```
